# Optimizing a Trainium2 kernel written in Bass

```python
import math
import jax, jax.numpy as jnp
from jax import lax
import numpy as np

D_MODEL = 4096
BATCH = 32
SEQ = 256
DEPTH = 2
DEC_BATCH = 8
DEC_SEQ = 1024
PAST_LEN = 512

GRID_W = 64
WIN_H = 8
WIN_W = 16
ATTN_WIDTH = D_MODEL // 2
HYENA_WIDTH = D_MODEL // 2
HEAD_DIM = 128
N_HEADS_A = ATTN_WIDTH // HEAD_DIM
SHORT_CONV = 3
FILTER_EMB = 33
FILTER_HIDDEN = 64
HYENA_ORDER = 2
DECAY_TARGET = 1e-2
FAST_DECAY_PCT = 0.3
SLOW_DECAY_PCT = 1.5
FNET_GROUPS = 8
D_FF = 11008
N_EVEN = (DEPTH + 1) // 2
N_ODD = DEPTH // 2
N_MOD = 9
CTX_BLOCK = 128
EPS = 1e-6
NEG_INF = -1e30

kernel_name = "hybrid_natten_hyena_fnet_prefix_step"


def _rmsnorm(x, g):
    xf = x.astype(jnp.float32)
    y = xf * lax.rsqrt(jnp.mean(xf * xf, axis=-1, keepdims=True) + EPS)
    return (y * g.astype(jnp.float32)).astype(x.dtype)


def _adaln(cond, w_mod, b_mod):
    m = jax.nn.silu(cond) @ w_mod + b_mod
    return m.reshape(cond.shape[0], N_MOD, cond.shape[-1])


def _mod_norm(x, g, mod, i):
    return _rmsnorm(x, g) * (1.0 + mod[:, None, i + 1]) + mod[:, None, i]


def _swiglu(h, w1, w3, w2):
    return (jax.nn.silu(h @ w1) * (h @ w3)) @ w2


def _ffn_half(x, g, mod, i, w1, w3, w2):
    h = _mod_norm(x, g, mod, i)
    return x + 0.5 * mod[:, None, i + 2] * _swiglu(h, w1, w3, w2)


def _heads(p, i):
    b, l, _ = p.shape
    return p[..., i * ATTN_WIDTH:(i + 1) * ATTN_WIDTH].reshape(b, l, N_HEADS_A, HEAD_DIM)


def _ctx_attention(q, k, v):
    b, lc, h, dh = q.shape
    nb = lc // CTX_BLOCK
    qb = jnp.moveaxis(q.reshape(b, nb, CTX_BLOCK, h, dh), 1, 0)
    scale = dh ** -0.5

    def block(q_blk):
        s = jnp.einsum("bqhd,bkhd->bhqk", q_blk, k, preferred_element_type=jnp.float32) * scale
        p = jax.nn.softmax(s, axis=-1).astype(v.dtype)
        return jnp.einsum("bhqk,bkhd->bqhd", p, v)

    o = lax.map(block, qb)
    return jnp.moveaxis(o, 0, 1).reshape(b, lc, h * dh)


def _na_latent(q, k, v, k_ctx, v_ctx, rpb):
    b, l, h, dh = q.shape
    rows = l // GRID_W
    kh = min(WIN_H, rows)
    n_lat = kh * GRID_W
    scale = dh ** -0.5
    qg = jnp.moveaxis(q.reshape(b, rows, GRID_W, h, dh), 1, 0)
    kg = k.reshape(b, rows, GRID_W, h, dh)
    vg = v.reshape(b, rows, GRID_W, h, dh)
    col = jnp.arange(GRID_W)
    col_start = jnp.clip(col - WIN_W // 2, 0, GRID_W - WIN_W)
    col_ok = (col[None, :] >= col_start[:, None]) & (col[None, :] < col_start[:, None] + WIN_W)
    dc_idx = jnp.clip(col[None, :] - col[:, None] + WIN_W - 1, 0, 2 * WIN_W - 2)
    rpb32 = rpb.astype(jnp.float32)

    def row_block(args):
        r, q_r = args
        start = jnp.clip(r - kh // 2, 0, rows - kh)
        k_r = lax.dynamic_slice_in_dim(kg, start, kh, axis=1)
        v_r = lax.dynamic_slice_in_dim(vg, start, kh, axis=1)
        dr_idx = start + jnp.arange(kh) - r + WIN_H - 1
        bias = jnp.transpose(rpb32[:, dr_idx][:, :, dc_idx], (0, 2, 1, 3))
        s_lat = jnp.einsum("bqhd,bikhd->bhqik", q_r, k_r, preferred_element_type=jnp.float32) * scale + bias
        s_lat = jnp.where(col_ok[:, None, :], s_lat, NEG_INF).reshape(b, h, GRID_W, n_lat)
        s_ctx = jnp.einsum("bqhd,bchd->bhqc", q_r, k_ctx, preferred_element_type=jnp.float32) * scale
        p = jax.nn.softmax(jnp.concatenate([s_lat, s_ctx], axis=-1), axis=-1).astype(v.dtype)
        o_lat = jnp.einsum("bhqn,bnhd->bqhd", p[..., :n_lat], v_r.reshape(b, n_lat, h, dh))
        o_ctx = jnp.einsum("bhqc,bchd->bqhd", p[..., n_lat:], v_ctx)
        return o_lat + o_ctx

    o = lax.map(row_block, (jnp.arange(rows), qg))
    return jnp.moveaxis(o, 0, 1).reshape(b, l, h * dh)


def _hyena_filter_freq(l, w1, b1, w2, b2, w3, freq):
    f32 = jnp.float32
    n_bands = (FILTER_EMB - 1) // 2
    t = jnp.linspace(0.0, 1.0, l, dtype=f32)[:, None]
    w = (2.0 * math.pi / l) * jnp.arange(l, dtype=f32)[:, None]
    bands = jnp.linspace(1e-4, n_bands - 1, n_bands, dtype=f32)[None, :]
    z = jnp.concatenate([t, jnp.cos(bands * w), -jnp.sin(bands * w)], axis=-1)
    hf = jnp.sin(freq[0].astype(f32) * (z @ w1.astype(f32) + b1.astype(f32)))
    hf = jnp.sin(freq[1].astype(f32) * (hf @ w2.astype(f32) + b2.astype(f32)))
    hf = (hf @ w3.astype(f32)).reshape(l, HYENA_ORDER, 2, HYENA_WIDTH)
    deltas = jnp.abs(jnp.linspace(math.log(DECAY_TARGET) / FAST_DECAY_PCT,
                                  math.log(DECAY_TARGET) / SLOW_DECAY_PCT, HYENA_WIDTH, dtype=f32))
    hf = hf * jnp.exp(-t[:, :, None, None] * deltas)
    h_fwd, h_bwd = hf[:, :, 0], hf[:, :, 1]
    filt = jnp.concatenate([h_fwd, jnp.zeros_like(h_fwd[:1]), h_bwd[1:][::-1]], axis=0)
    filt = filt * lax.rsqrt(jnp.sum(filt * filt, axis=0, keepdims=True) + EPS)
    return jnp.fft.rfft(filt, axis=0)


def _long_conv(u, k_f, bias):
    l = u.shape[1]
    y = jnp.fft.irfft(jnp.fft.rfft(u, n=2 * l, axis=1) * k_f, n=2 * l, axis=1)[:, :l]
    return y + u * bias


def _short_conv(u, w, b):
    l = u.shape[1]
    pad = SHORT_CONV // 2
    up = jnp.pad(u, ((0, 0), (pad, pad), (0, 0)))
    out = b + up[:, 0:l] * w[0]
    for j in range(1, SHORT_CONV):
        out = out + up[:, j:j + l] * w[j]
    return out


def _hyena(p, w_short, b_short, fw1, fb1, fw2, fb2, fw3, ffreq, fbias):
    l = p.shape[1]
    u = _short_conv(p, w_short, b_short).astype(jnp.float32)
    v, x1, x2 = jnp.split(u, 3, axis=-1)
    k_f = _hyena_filter_freq(l, fw1, fb1, fw2, fb2, fw3, ffreq)
    bias = fbias.astype(jnp.float32)
    z = x1 * _long_conv(v, k_f[:, 0], bias[0])
    z = x2 * _long_conv(z, k_f[:, 1], bias[1])
    return z.astype(p.dtype)


def _fnet(h, w_c):
    b, l, d = h.shape
    hg = h.astype(jnp.float32).reshape(b, l, FNET_GROUPS, d // FNET_GROUPS)
    mixed = jnp.real(jnp.fft.fft2(hg, axes=(1, 3), norm="ortho")).reshape(b, l, d)
    return mixed.astype(h.dtype) @ w_c


def setup_inputs(seed: int = 0) -> dict:
    key = jax.random.key(seed)
    ks = jax.random.split(key, 32)
    f32 = jnp.float32
    D = D_MODEL
    P = 3 * ATTN_WIDTH + 3 * HYENA_WIDTH

    def nrm(k, shape, s):
        return jax.random.normal(k, shape, f32) * s

    return {
        "x_prompt": nrm(ks[0], (BATCH, SEQ, D), 1.0),
        "x_sample": nrm(ks[1], (DEC_BATCH, DEC_SEQ, D), 1.0),
        "cache_k": nrm(ks[2], (DEC_BATCH, N_EVEN, PAST_LEN, N_HEADS_A, HEAD_DIM), 1.0),
        "cache_v": nrm(ks[3], (DEC_BATCH, N_EVEN, PAST_LEN, N_HEADS_A, HEAD_DIM), 1.0),
        "c": nrm(ks[4], (DEC_BATCH, D), 1.0),
        "c_ctx": nrm(ks[5], (D,), 1.0),
        "w_mod": nrm(ks[6], (DEPTH, D, N_MOD * D), 0.5 * D ** -0.5),
        "b_mod": nrm(ks[7], (DEPTH, N_MOD * D), 0.02),
        "norm_g": 1.0 + nrm(ks[8], (DEPTH, 3, D), 0.02),
        "ffn_w1": nrm(ks[9], (DEPTH, 2, D, D_FF), D ** -0.5),
        "ffn_w3": nrm(ks[10], (DEPTH, 2, D, D_FF), D ** -0.5),
        "ffn_w2": nrm(ks[11], (DEPTH, 2, D_FF, D), D_FF ** -0.5),
        "w_in": nrm(ks[12], (N_EVEN, D, P), D ** -0.5),
        "w_out": nrm(ks[13], (N_EVEN, ATTN_WIDTH + HYENA_WIDTH, D), (ATTN_WIDTH + HYENA_WIDTH) ** -0.5),
        "rpb": nrm(ks[14], (N_EVEN, N_HEADS_A, 2 * WIN_H - 1, 2 * WIN_W - 1), 0.02),
        "w_short": nrm(ks[15], (N_EVEN, SHORT_CONV, 3 * HYENA_WIDTH), SHORT_CONV ** -0.5),
        "b_short": nrm(ks[16], (N_EVEN, 3 * HYENA_WIDTH), 0.02),
        "filt_w1": nrm(ks[17], (N_EVEN, FILTER_EMB, FILTER_HIDDEN), FILTER_EMB ** -0.5),
        "filt_b1": nrm(ks[18], (N_EVEN, FILTER_HIDDEN), 0.02),
        "filt_w2": nrm(ks[19], (N_EVEN, FILTER_HIDDEN, FILTER_HIDDEN), FILTER_HIDDEN ** -0.5),
        "filt_b2": nrm(ks[20], (N_EVEN, FILTER_HIDDEN), 0.02),
        "filt_w3": nrm(ks[21], (N_EVEN, FILTER_HIDDEN, HYENA_ORDER * 2 * HYENA_WIDTH), FILTER_HIDDEN ** -0.5),
        "filt_freq": 1.0 + nrm(ks[22], (N_EVEN, 2, FILTER_HIDDEN), 0.02),
        "filt_bias": nrm(ks[23], (N_EVEN, HYENA_ORDER, HYENA_WIDTH), 0.5),
        "w_fnet": nrm(ks[24], (N_ODD, D, D), D ** -0.5),
        "final_g": 1.0 + nrm(ks[25], (D,), 0.02),
    }


def reference(x_prompt, x_sample, cache_k, cache_v, c, c_ctx, w_mod, b_mod, norm_g, ffn_w1, ffn_w3, ffn_w2,
              w_in, w_out, rpb, w_short, b_short, filt_w1, filt_b1, filt_w2, filt_b2, filt_w3, filt_freq,
              filt_bias, w_fnet, final_g):
    xp, xs = x_prompt, x_sample
    new_k, new_v = [], []
    for l in range(DEPTH):
        mc = _adaln(c_ctx[None, :], w_mod[l], b_mod[l])
        ml = _adaln(c, w_mod[l], b_mod[l])
        xp = _ffn_half(xp, norm_g[l, 0], mc, 0, ffn_w1[l, 0], ffn_w3[l, 0], ffn_w2[l, 0])
        xs = _ffn_half(xs, norm_g[l, 0], ml, 0, ffn_w1[l, 0], ffn_w3[l, 0], ffn_w2[l, 0])
        hp = _mod_norm(xp, norm_g[l, 1], mc, 3)
        hs = _mod_norm(xs, norm_g[l, 1], ml, 3)
        if l % 2 == 0:
            e = l // 2
            pp = hp @ w_in[e]
            ps = hs @ w_in[e]
            kp, vp = _heads(pp, 1), _heads(pp, 2)
            a_p = _ctx_attention(_heads(pp, 0), kp, vp)
            a_s = _na_latent(_heads(ps, 0), _heads(ps, 1), _heads(ps, 2), cache_k[:, e], cache_v[:, e], rpb[e])
            hy_p = _hyena(pp[..., 3 * ATTN_WIDTH:], w_short[e], b_short[e], filt_w1[e], filt_b1[e],
                          filt_w2[e], filt_b2[e], filt_w3[e], filt_freq[e], filt_bias[e])
            hy_s = _hyena(ps[..., 3 * ATTN_WIDTH:], w_short[e], b_short[e], filt_w1[e], filt_b1[e],
                          filt_w2[e], filt_b2[e], filt_w3[e], filt_freq[e], filt_bias[e])
            mix_p = jnp.concatenate([a_p, hy_p], axis=-1) @ w_out[e]
            mix_s = jnp.concatenate([a_s, hy_s], axis=-1) @ w_out[e]
            new_k.append(kp)
            new_v.append(vp)
        else:
            o = l // 2
            mix_p = _fnet(hp, w_fnet[o])
            mix_s = _fnet(hs, w_fnet[o])
        xp = xp + mc[:, None, 5] * mix_p
        xs = xs + ml[:, None, 5] * mix_s
        xp = _ffn_half(xp, norm_g[l, 2], mc, 6, ffn_w1[l, 1], ffn_w3[l, 1], ffn_w2[l, 1])
        xs = _ffn_half(xs, norm_g[l, 2], ml, 6, ffn_w1[l, 1], ffn_w3[l, 1], ffn_w2[l, 1])
    y_prompt = _rmsnorm(xp, final_g)
    y_sample = _rmsnorm(xs, final_g)
    return (y_prompt, y_sample, jnp.stack(new_k, axis=1), jnp.stack(new_v, axis=1))
```

```python
import contextlib
import math
import numpy as np
import ml_dtypes
import concourse.bass as bass
import concourse.mybir as mybir
from concourse.bass_utils import run_bass_kernel_spmd

F32 = mybir.dt.float32
BF16 = mybir.dt.bfloat16
AF = mybir.ActivationFunctionType
ALU = mybir.AluOpType
AX = mybir.AxisListType

NCORES = 8
D = 4096
KC = 32
DFF = 11008
JF = 86
NTOK = 2048
TT = 512
NTT = 4
EPS = 1e-6
NEG = -1e30


class DSem:
    def __init__(self, sem):
        self.sem = sem
        self.cnt = 0


class Buf:
    __slots__ = ("name", "last_write", "readers", "dsem")

    def __init__(self, name, dsem=None):
        self.name = name
        self.last_write = None
        self.readers = {}
        self.dsem = dsem


class Sched:
    ENGS = ("pe", "act", "dve", "pool", "sp")

    def __init__(self, nc, stack):
        self.nc = nc
        self.stack = stack
        self.eng = {"pe": nc.tensor, "act": nc.scalar, "dve": nc.vector, "pool": nc.gpsimd, "sp": nc.sync}
        self.sem = {}
        self.cnt = {}
        self.nsem = 0
        for e in ("pe", "act", "dve", "pool"):
            self.sem[e] = self._newsem("e_" + e)
            self.cnt[e] = 0
        self.seen = {e: {} for e in self.ENGS}
        self.dsems = []
        self.ninstr = {e: 0 for e in self.ENGS}

    def _newsem(self, name):
        self.nsem += 1
        return self.stack.enter_context(self.nc.semaphore(f"{name}_{self.nsem}"))

    def new_dsem(self, name="d"):
        d = DSem(self._newsem(name))
        self.dsems.append(d)
        return d

    def _wait_token(self, engine, tok):
        if tok is None:
            return
        if tok[0] == "c":
            _, e, v = tok
            if e == engine and engine == "pe":
                return
            sem = self.sem[e]
            key = ("c", e)
        else:
            d = tok[1]
            sem = d.sem
            v = 16 * d.cnt
            key = ("d", id(d))
        if self.seen[engine].get(key, 0) >= v:
            return
        self.seen[engine][key] = v
        self.eng[engine].wait_ge(sem, v)
        self.ninstr[engine] += 1

    def _deps(self, engine, reads, writes):
        for b in reads:
            self._wait_token(engine, b.last_write)
        for b in writes:
            self._wait_token(engine, b.last_write)
            for t in list(b.readers.values()):
                self._wait_token(engine, t)

    def _commit(self, tok, key, reads, writes):
        for b in reads:
            b.readers[key] = tok
        for b in writes:
            b.last_write = tok
            b.readers = {}

    def op(self, engine, fn, reads=(), writes=()):
        psr = [b for b in reads if b.name.startswith("ps")]
        if psr:
            writes = list(writes) + psr
        self._deps(engine, reads, writes)
        ins = fn(self.eng[engine])
        self.ninstr[engine] += 1
        self.cnt[engine] += 1
        ins.then_inc(self.sem[engine], 1)
        self._commit(("c", engine, self.cnt[engine]), ("c", engine), reads, writes)
        return ins

    def mm(self, out_ap, pairs, reads=(), writes=(), start=True, stop=True):
        self._deps("pe", reads, writes)
        n = len(pairs)
        pe = self.eng["pe"]
        ins = None
        for i, (l, r) in enumerate(pairs):
            ins = pe.matmul(out_ap, l, r, start=(start and i == 0), stop=(stop and i == n - 1))
        self.ninstr["pe"] += n
        self.cnt["pe"] += 1
        ins.then_inc(self.sem["pe"], 1)
        self._commit(("c", "pe", self.cnt["pe"]), ("c", "pe"), reads, writes)

    def tr(self, out_ap, in_ap, ident_ap, reads=(), writes=()):
        self._deps("pe", reads, writes)
        ins = self.eng["pe"].transpose(out_ap, in_ap, ident_ap)
        self.ninstr["pe"] += 1
        self.cnt["pe"] += 1
        ins.then_inc(self.sem["pe"], 1)
        self._commit(("c", "pe", self.cnt["pe"]), ("c", "pe"), reads, writes)

    def dma(self, out_ap, in_ap, reads=(), writes=(), queue="sp"):
        self._deps(queue, reads, writes)
        wb = writes[0]
        if wb.dsem is None:
            if not hasattr(self, "_pool"):
                self._pool = [self.new_dsem(f"pool{i}") for i in range(40)]
                self._pool_i = 0
            wb.dsem = self._pool[self._pool_i % len(self._pool)]
            self._pool_i += 1
        d = wb.dsem
        ins = self.eng[queue].dma_start(out=out_ap, in_=in_ap)
        ins.then_inc(d.sem, 16)
        d.cnt += 1
        self.ninstr[queue] += 1
        self._commit(("d", d), ("d", id(d)), reads, writes)

    def barrier(self):
        for e in self.ENGS:
            for e2 in ("pe", "act", "dve", "pool"):
                if self.cnt[e2] > 0:
                    self._wait_token(e, ("c", e2, self.cnt[e2]))
            for d in self.dsems:
                if d.cnt > 0:
                    self._wait_token(e, ("d", d))


def _bf(a):
    return np.ascontiguousarray(np.asarray(a, np.float32).astype(ml_dtypes.bfloat16))


def _dft_consts(l):
    f = np.arange(l)[:, None].astype(np.float64)
    s = np.arange(2 * l)[None, :].astype(np.float64)
    cosb = np.cos(np.pi * f * s / l)
    sinb = -np.sin(np.pi * f * s / l)
    sinb[0, :] = np.cos(np.pi * s[0])
    fwd = np.concatenate([cosb, sinb], axis=0)
    FT = fwd[:, :l].T.copy()
    wgt = np.full((2 * l, 1), 1.0 / l)
    wgt[0, 0] = 0.5 / l
    wgt[l, 0] = 0.5 / l
    GI = (fwd[:, :l] * wgt).copy()
    return FT, GI


def _filter_feats(l):
    n_bands = 16
    t = np.linspace(0.0, 1.0, l, dtype=np.float32)
    w = (np.float32(2.0 * math.pi / l) * np.arange(l, dtype=np.float32))
    bands = np.linspace(1e-4, n_bands - 1, n_bands, dtype=np.float32)
    z = np.concatenate([t[:, None], np.cos(bands[None, :] * w[:, None]), -np.sin(bands[None, :] * w[:, None])], axis=-1)
    idx = np.zeros(2 * l, np.int64)
    idx[:l] = np.arange(l)
    idx[l] = 0
    idx[l + 1:] = np.arange(l - 1, 0, -1)
    zT = z[idx].T.astype(np.float32).copy()
    tx = t[idx].astype(np.float32).copy()
    tx[l] = 1e4
    return zT, tx


def _consts():
    c = {}
    c["ident"] = np.eye(128, dtype=np.float32)
    c["identb"] = _bf(np.eye(128))
    c["onesb"] = _bf(np.ones((128, 128)))
    col = np.arange(64)
    cs = np.clip(col - 8, 0, 48)
    ok = (col[None, :] >= cs[:, None]) & (col[None, :] < cs[:, None] + 16)
    c["maskc"] = np.where(ok, 0.0, NEG).astype(np.float32)
    deltas = np.abs(np.linspace(math.log(1e-2) / 0.3, math.log(1e-2) / 1.5, 2048, dtype=np.float32))
    c["deltas"] = deltas.reshape(1, 2048).astype(np.float32)
    for l in (256, 1024):
        FT, GI = _dft_consts(l)
        c[f"FT{l}"] = _bf(FT.reshape(l // 128, 128, 2 * l).transpose(1, 0, 2))
        c[f"GI{l}"] = _bf(GI.reshape(2 * l // 128, 128, l).transpose(1, 0, 2))
        zT, tx = _filter_feats(l)
        c[f"zT{l}"] = zT
        c[f"ntx{l}"] = np.ascontiguousarray((-tx).reshape(2 * l // 128, 128).T)
        sg = np.where(np.arange(128) % 2 == 0, 1.0, -1.0).astype(np.float32)
        t = np.arange(l)[:, None].astype(np.float64)
        sc = 1.0 / math.sqrt(l * 512.0)
        CL = np.cos(2 * np.pi * t * t.T / l) * sc
        SL = -np.sin(2 * np.pi * t * t.T / l) * sc
        c[f"CL{l}"] = _bf(CL.reshape(l // 128, 128, l).transpose(1, 0, 2))
        c[f"SL{l}"] = _bf(SL.reshape(l // 128, 128, l).transpose(1, 0, 2))
    c["sgn"] = np.where(np.arange(128) % 2 == 0, 1.0, -1.0).astype(np.float32).reshape(128, 1)
    cc = np.arange(512)[:, None].astype(np.float64)
    CC = np.cos(2 * np.pi * cc * cc.T / 512)
    SC = np.sin(2 * np.pi * cc * cc.T / 512)
    CS = np.concatenate([CC, SC], axis=1)
    c["CS"] = _bf(CS.reshape(4, 128, 1024).transpose(1, 0, 2))
    return c


CONST_SPECS = None


def build_program(stages=("all",), dbg=(), tiles=None, lite=False):
    nc = bass.Bass("TRN2", target_bir_lowering=False)
    TILES = list(range(NTT)) if tiles is None else list(tiles)
    ALL = "all" in stages

    def din(name, shape, dt=F32):
        return nc.dram_tensor(name, list(shape), dt, kind="ExternalInput").ap()

    def dout(name, shape, dt=F32):
        return nc.dram_tensor(name, list(shape), dt, kind="ExternalOutput").ap()

    def dscr(name, shape, dt=F32):
        return nc.dram_tensor(name, list(shape), dt, kind="Internal").ap()

    xp_d = din("xp", [1024, D])
    xs_d = din("xs", [1024, D])
    ck_d = din("ck", [512, 2048])
    cv_d = din("cv", [512, 2048])
    condT_d = din("condT", [128, 64])
    has_ffn = ALL or any(x.startswith("ffn") for x in stages)
    has_mod = ALL or "mod" in stages
    if lite and not has_ffn:
        W13_d = din("W13", [1, 1, 2, 128, D])
        W2_d = din("W2T", [1, 1, 128, JF * 128])
    else:
        W13_d = din("W13", [4, JF, 2, 128, D])
        W2_d = din("W2T", [4, KC, 128, JF * 128])
    WIN_d = din("WIN", [96, 128, D])
    WOUT_d = din("WOUT", [32, 128, D])
    WFN_d = din("WFN", [32, 128, D])
    if lite and not has_mod:
        WMOD_d = din("WMOD", [1, 1, 128, D])
    else:
        WMOD_d = din("WMOD", [2, 288, 128, D])
    bmodT_d = din("bmodT", [2, 128, 288])
    gT_d = din("gT", [128, 7 * 32])
    wsT_d = din("wsT", [128, 48 * 3])
    bsT_d = din("bsT", [128, 48])
    fw1_d = din("fw1", [33, 64])
    fb1_d = din("fb1", [64, 1])
    fw2_d = din("fw2", [64, 64])
    fb2_d = din("fb2", [64, 1])
    ffq_d = din("ffq", [64, 2])
    fw3_d = din("fw3", [64, 8192])
    fbias_d = din("fbias", [2, 2048])
    rpbp_d = din("rpbp", [16, 15, 127])
    tzh_d = din("tzh", [64, 16 * 15 * 64])
    cst = {}
    cshapes = {"ident": ([128, 128], F32), "identb": ([128, 128], BF16), "onesb": ([128, 128], BF16),
               "maskc": ([64, 64], F32), "deltas": ([1, 2048], F32), "sgn": ([128, 1], F32), "CS": ([128, 4, 1024], BF16)}
    for l in (256, 1024):
        cshapes[f"FT{l}"] = ([128, l // 128, 2 * l], BF16)
        cshapes[f"GI{l}"] = ([128, 2 * l // 128, l], BF16)
        cshapes[f"zT{l}"] = ([33, 2 * l], F32)
        cshapes[f"ntx{l}"] = ([128, 2 * l // 128], F32)
        cshapes[f"CL{l}"] = ([128, l // 128, l], BF16)
        cshapes[f"SL{l}"] = ([128, l // 128, l], BF16)
    for k, (shp, dt) in cshapes.items():
        cst[k] = din("c_" + k, shp, dt)

    yp_d = dout("yp", [1024, D])
    ys_d = dout("ys", [1024, D])
    nk_d = dout("nk", [1024, 2048])
    nv_d = dout("nv", [1024, 2048])

    XT = dscr("XT", [D, NTOK])
    QKT = dscr("QKT", [D, NTOK], BF16)
    VT = dscr("VT", [NTOK, 2048], BF16)
    HT = dscr("HT", [6144 + 4096, NTOK])
    MIXT = dscr("MIXT", [D, NTOK], BF16)
    FA = dscr("FA", [NTOK, 8, 1024], BF16)
    dbg_out = {}

    with contextlib.ExitStack() as gst:
        S = Sched(nc, gst)

        def sb(st, name, shape, dt):
            return st.enter_context(nc.sbuf_tensor("s_" + name, list(shape), dt))

        PS = gst.enter_context(nc.psum_tensor("PS", [128, 7, 512], F32))
        PSB = gst.enter_context(nc.psum_tensor("PSB", [128, 1024], BF16))
        Bps = [Buf(f"ps{i}") for i in range(7)]
        BpsB = Buf("psB")

        ds_xt = [S.new_dsem(f"xt{t}") for t in range(NTT)]
        B_XT = [[Buf(f"XT{t}_{k}", ds_xt[t]) for k in range(KC)] for t in range(NTT)]
        ds_misc = S.new_dsem("scr")
        B_QKT = [Buf(f"QKT{t}", S.new_dsem(f"qkt{t}")) for t in range(NTT)]
        B_VT = [Buf(f"VT{t}", S.new_dsem(f"vt{t}")) for t in range(NTT)]
        B_HT = [Buf(f"HT{t}", S.new_dsem(f"ht{t}")) for t in range(NTT)]
        B_MIXT = [Buf(f"MIXT{t}", S.new_dsem(f"mixt{t}")) for t in range(NTT)]
        B_FA = [Buf(f"FA{t}", S.new_dsem(f"fa{t}")) for t in range(NTT)]
        B_out = Buf("outs", S.new_dsem("outs"))

        ident = sb(gst, "ident", [128, 128], F32)
        identb = sb(gst, "identb", [128, 128], BF16)
        onesb = sb(gst, "onesb", [128, 128], BF16)
        gT = sb(gst, "gT", [128, 7 * 32], F32)
        modp = sb(gst, "modp", [128, 2 * 3 * 2 * 3 * 32], F32)
        B_const = Buf("const")
        B_modp = Buf("modp")
        S.dma(ident[:], cst["ident"], writes=[B_const])
        S.dma(identb[:], cst["identb"], writes=[B_const])
        S.dma(onesb[:], cst["onesb"], writes=[B_const])
        S.dma(gT[:], gT_d, writes=[B_const])

        def mp(l, i3, v, which):
            o = ((((l * 3 + i3) * 2 + v) * 3) + which) * 32
            return modp[:, o:o + 32]

        XTv = XT.rearrange("(k p) t -> k p t", p=128)

        def alloc_stream(st, tag, fp32=False, NST=3, NWB=4, unit=16):
            ss = {"fp32": fp32, "NST": NST, "NWB": NWB, "unit": unit, "ctr": 0}
            ss["stg"] = [sb(st, f"{tag}_stg{i}", [128, unit * 128], F32) for i in range(NST)]
            ss["B_stg"] = [Buf(f"{tag}_stg{i}") for i in range(NST)]
            if not fp32:
                ss["wb"] = [sb(st, f"{tag}_wb{i}", [128, unit * 128], BF16) for i in range(NWB)]
                ss["B_wb"] = [Buf(f"{tag}_wb{i}") for i in range(NWB)]
            return ss

        cast_rr = ["dve", "pool", "dve", "act"]

        def stream_linear(ss, w_tile_ap, J, nblk, rhs_fn, rhs_bufs, N, evac, ps_banks, M=128):
            unit, NST, NWB, fp32 = ss["unit"], ss["NST"], ss["NWB"], ss["fp32"]
            stg, B_stg = ss["stg"], ss["B_stg"]
            units = []
            b0 = 0
            nun = (nblk + unit - 1) // unit
            base, rem = divmod(nblk, nun)
            for u in range(nun):
                nb = base + (1 if u < rem else 0)
                units.append((b0, nb))
                b0 += nb
            for j in range(J):
                bank = ps_banks[j % len(ps_banks)]
                wap = w_tile_ap(j)
                for ui, (b0, nb) in enumerate(units):
                    c = ss["ctr"]
                    ss["ctr"] += 1
                    si = c % NST
                    S.dma(stg[si][:, :nb * 128], wap[:, b0 * 128:(b0 + nb) * 128], writes=[B_stg[si]])
                    if fp32:
                        src, Bsrc = stg[si], B_stg[si]
                    else:
                        wi = c % NWB
                        wbt, Bw = ss["wb"][wi], ss["B_wb"][wi]
                        ce = cast_rr[c % 4]
                        if ce == "act":
                            S.op("act", lambda e: e.copy(out=wbt[:, :nb * 128], in_=stg[si][:, :nb * 128]),
                                 reads=[B_stg[si]], writes=[Bw])
                        else:
                            S.op(ce, lambda e: e.tensor_copy(out=wbt[:, :nb * 128], in_=stg[si][:, :nb * 128]),
                                 reads=[B_stg[si]], writes=[Bw])
                        src, Bsrc = wbt, Bw
                    pairs = [(src[:, q * 128:q * 128 + M], rhs_fn(b0 + q)) for q in range(nb)]
                    S.mm(PS[:M, bank, :N], pairs, reads=[Bsrc] + list(rhs_bufs), writes=[Bps[bank]],
                         start=(ui == 0), stop=(ui == len(units) - 1))
                evac(j, bank)

        ones32 = sb(gst, "ones32", [128, 128], F32)
        S.op("dve", lambda e: e.memset(ones32[:], 1.0), writes=[B_const])

        def load_modnorm(st_bufs, tt, gs_ap, sh_ap, f32stats=False):
            (xs4, B_xs4, sq2, B_sq2, rstd, B_rstd, hT, B_hT) = st_bufs
            t0 = tt * TT
            for k in range(KC):
                i = k % 4
                S.dma(xs4[i][:], XTv[k, :, t0:t0 + TT], reads=[B_XT[tt][k]], writes=[B_xs4[i]])
                q = k % 2
                S.op("act", lambda e: e.activation(out=sq2[q][:], in_=xs4[i][:], func=AF.Square),
                     reads=[B_xs4[i]], writes=[B_sq2[q]])
                S.mm(PS[:, 6, :], [((ones32 if f32stats else onesb)[:], sq2[q][:])], reads=[B_sq2[q], B_const], writes=[Bps[6]],
                     start=(k == 0), stop=(k == KC - 1))
            S.op("act", lambda e: e.activation(out=rstd[:], in_=PS[:, 6, :], func=AF.Sqrt, bias=EPS, scale=1.0 / D),
                 reads=[Bps[6]], writes=[B_rstd])
            S.op("dve", lambda e: e.reciprocal(out=rstd[:], in_=rstd[:]), reads=[B_rstd], writes=[B_rstd])
            if hT is None:
                return
            for k in range(KC):
                i = k % 4
                S.dma(xs4[i][:], XTv[k, :, t0:t0 + TT], reads=[B_XT[tt][k]], writes=[B_xs4[i]])
                S.op("dve", lambda e: e.tensor_tensor(out=xs4[i][:], in0=xs4[i][:], in1=rstd[:], op=ALU.mult),
                     reads=[B_xs4[i], B_rstd], writes=[B_xs4[i]])
                S.op("act", lambda e: e.activation(out=hT[:, k, :], in_=xs4[i][:], func=AF.Identity,
                                                   bias=sh_ap[:, k:k + 1], scale=gs_ap[:, k:k + 1]),
                     reads=[B_xs4[i], B_modp], writes=[B_hT[k]])

        def alloc_modnorm(st, tag, with_h=True):
            xs4 = [sb(st, f"{tag}_xs{i}", [128, TT], F32) for i in range(4)]
            B_xs4 = [Buf(f"{tag}_xs{i}") for i in range(4)]
            sq2 = [sb(st, f"{tag}_sq{i}", [128, TT], BF16 if with_h else F32) for i in range(2)]
            B_sq2 = [Buf(f"{tag}_sq{i}") for i in range(2)]
            rstd = sb(st, f"{tag}_rstd", [128, TT], F32)
            B_rstd = Buf(f"{tag}_rstd")
            if with_h:
                hT = sb(st, f"{tag}_hT", [128, KC, TT], BF16)
                B_hT = [Buf(f"{tag}_hT{k}") for k in range(KC)]
            else:
                hT, B_hT = None, None
            return (xs4, B_xs4, sq2, B_sq2, rstd, B_rstd, hT, B_hT)

        def resid_evac(st_bufs, ost, B_ost, tt, gate_ap):
            (xs4, B_xs4) = st_bufs[0], st_bufs[1]
            t0 = tt * TT
            cnt = [0]

            pending = []

            def flush():
                while pending:
                    m_, o_ = pending.pop(0)
                    S.dma(XTv[m_, :, t0:t0 + TT], ost[o_][:], reads=[B_ost[o_]], writes=[B_XT[tt][m_]])

            def evac(m, bank):
                i = cnt[0] % 4
                o = cnt[0] % 2
                cnt[0] += 1
                S.dma(xs4[i][:], XTv[m, :, t0:t0 + TT], reads=[B_XT[tt][m]], writes=[B_xs4[i]])
                flush()
                S.op("dve", lambda e: e.scalar_tensor_tensor(out=ost[o][:], in0=PS[:, bank, :], scalar=gate_ap[:, m:m + 1],
                                                             in1=xs4[i][:], op0=ALU.mult, op1=ALU.add),
                     reads=[Bps[bank], B_xs4[i], B_modp], writes=[B_ost[o]])
                pending.append((m, o))
            evac.flush = flush
            return evac

        def stage_in():
            with contextlib.ExitStack() as st:
                xin = [sb(st, f"xin{i}", [128, D], F32) for i in range(2)]
                B_xin = [Buf(f"xin{i}") for i in range(2)]
                xo = [sb(st, f"xo{i}", [128, 4, 128], F32) for i in range(4)]
                B_xo = [Buf(f"xo{i}") for i in range(4)]
                c = 0
                for tb in range(16):
                    src = xp_d if tb < 8 else xs_d
                    r0 = (tb % 8) * 128
                    tt = tb // 4
                    i = tb % 2
                    S.dma(xin[i][:], src[r0:r0 + 128, :], writes=[B_xin[i]])
                    for k4 in range(8):
                        bank = k4 % 2
                        for q in range(4):
                            k = k4 * 4 + q
                            S.tr(PS[:, bank, q * 128:(q + 1) * 128], xin[i][:, k * 128:(k + 1) * 128], ident[:],
                                 reads=[B_xin[i], B_const], writes=[Bps[bank]])
                        o = c % 4
                        c += 1
                        eng = "dve" if c % 2 else "act"
                        if eng == "dve":
                            S.op("dve", lambda e: e.tensor_copy(out=xo[o][:].rearrange("p a b -> p (a b)"), in_=PS[:, bank, :]),
                                 reads=[Bps[bank]], writes=[B_xo[o]])
                        else:
                            S.op("act", lambda e: e.copy(out=xo[o][:].rearrange("p a b -> p (a b)"), in_=PS[:, bank, :]),
                                 reads=[Bps[bank]], writes=[B_xo[o]])
                        dst = XTv[k4 * 4:(k4 + 1) * 4, :, tb * 128:(tb + 1) * 128].rearrange("k p t -> p k t")
                        S.dma(dst, xo[o][:], reads=[B_xo[o]], writes=[B_XT[tt][k4 * 4]])
                        for q in range(1, 4):
                            B_XT[tt][k4 * 4 + q].last_write = B_XT[tt][k4 * 4].last_write
                S.barrier()

        def stage_mod():
            with contextlib.ExitStack() as st:
                condT = sb(st, "condT", [128, 64], F32)
                sT = sb(st, "sT", [128, 64], F32)
                bmT = sb(st, "bmT", [128, 288], F32)
                modT = sb(st, "modT", [128, 288, 2], F32)
                B_c, B_s, B_bm, B_mod = Buf("condT"), Buf("sT"), Buf("bmT"), Buf("modT")
                S.dma(condT[:], condT_d, writes=[B_c])
                S.op("act", lambda e: e.activation(out=sT[:], in_=condT[:], func=AF.Silu), reads=[B_c], writes=[B_s])
                for l in range(2):
                    S.dma(bmT[:], bmodT_d[l], writes=[B_bm])

                    def evac(j, bank, l=l):
                        S.op("dve", lambda e: e.tensor_scalar(out=modT[:, j, :], in0=PS[:, bank, 0:2], scalar1=bmT[:, j:j + 1],
                                                              scalar2=None, op0=ALU.add),
                             reads=[Bps[bank], B_bm], writes=[B_mod])

                    if l == 0:
                        ssm = alloc_stream(st, "modw", fp32=True, NST=4)
                    stream_linear(ssm, lambda j, l=l: WMOD_d[l, j], 288, KC,
                                  lambda kk: sT[:, kk * 2:kk * 2 + 2], [B_s], 2, evac, [0, 1, 2, 3])
                    for i3 in range(3):
                        for v in range(2):
                            shift = modT[:, (3 * i3) * 32:(3 * i3 + 1) * 32, v]
                            scale = modT[:, (3 * i3 + 1) * 32:(3 * i3 + 2) * 32, v]
                            gate = modT[:, (3 * i3 + 2) * 32:(3 * i3 + 3) * 32, v]
                            g = gT[:, (l * 3 + i3) * 32:(l * 3 + i3 + 1) * 32]
                            S.op("dve", lambda e: e.scalar_tensor_tensor(out=mp(l, i3, v, 0), in0=scale, scalar=1.0, in1=g,
                                                                         op0=ALU.add, op1=ALU.mult),
                                 reads=[B_mod, B_const], writes=[B_modp])
                            S.op("dve", lambda e: e.tensor_copy(out=mp(l, i3, v, 1), in_=shift), reads=[B_mod], writes=[B_modp])
                            S.op("dve", lambda e: e.tensor_scalar(out=mp(l, i3, v, 2), in0=gate, scalar1=(1.0 if i3 == 1 else 0.5),
                                                                  scalar2=None, op0=ALU.mult),
                                 reads=[B_mod], writes=[B_modp])
                S.barrier()

        def stage_ffn(l, i):
            li = l * 2 + i
            i3 = 0 if i == 0 else 2
            with contextlib.ExitStack() as st:
                mn = alloc_modnorm(st, f"f{li}")
                hT, B_hT = mn[6], mn[7]
                aT = sb(st, f"f{li}_aT", [128, JF, TT], BF16)
                B_aT = [Buf(f"aT{j}") for j in range(JF)]
                sg = [sb(st, f"f{li}_sg{q}", [128, TT], F32) for q in range(2)]
                B_sg = [Buf(f"sg{q}") for q in range(2)]
                ost = [sb(st, f"f{li}_ost{q}", [128, TT], F32) for q in range(2)]
                B_ost = [Buf(f"ost{q}") for q in range(2)]
                ssw = alloc_stream(st, f"f{li}w", NST=5, NWB=5)
                if True:
                    for tt in TILES:
                        v = 0 if tt < 2 else 1
                        load_modnorm(mn, tt, mp(l, i3, v, 0), mp(l, i3, v, 1))
                        cnt = [0]

                        def evac1(jj, bank):
                            j, w = divmod(jj, 2)
                            if w == 0:
                                return
                            q = j % 2
                            gb, ub = bank - 1, bank
                            S.op("act", lambda e: e.activation(out=sg[q][:], in_=PS[:, gb, :], func=AF.Silu),
                                 reads=[Bps[gb]], writes=[B_sg[q]])
                            S.op("dve", lambda e: e.tensor_tensor(out=aT[:, j, :], in0=sg[q][:], in1=PS[:, ub, :], op=ALU.mult),
                                 reads=[B_sg[q], Bps[ub]], writes=[B_aT[j]])

                        stream_linear(ssw, lambda jj: W13_d[li, jj // 2, jj % 2], 2 * JF, KC,
                                      lambda kk: hT[:, kk, :], B_hT, TT, evac1, [0, 1, 2, 3])
                        ev2 = resid_evac(mn, ost, B_ost, tt, mp(l, i3, v, 2))
                        stream_linear(ssw, lambda m: W2_d[li, m], KC, JF,
                                      lambda jj: aT[:, jj, :], B_aT, TT, ev2, [4, 5])
                        ev2.flush()
                S.barrier()

        QKv = QKT.rearrange("(h p) t -> p h t", p=128)
        VTv = VT.rearrange("(b p) c -> p b c", p=128)
        MIXv = MIXT.rearrange("(h p) t -> p h t", p=128)
        HTv = HT.rearrange("(j p) t -> p j t", p=128)
        nkv = nk_d.rearrange("(b p) c -> p b c", p=128)
        nvv = nv_d.rearrange("(b p) c -> p b c", p=128)
        QSCALE = float(128 ** -0.5)

        nwq = [S.new_dsem(f"nw{i}") for i in range(6)]
        nwi = [0]

        def dma_nw(out_ap, in_ap, reads, wbuf):
            S.dma(out_ap, in_ap, reads=reads, writes=[wbuf])
            return
            d = nwq[nwi[0] % len(nwq)]
            nwi[0] += 1
            S._deps("sp", reads, ())
            if d.cnt > 0:
                S._wait_token("sp", ("d", d))
            ins = S.eng["sp"].dma_start(out=out_ap, in_=in_ap)
            ins.then_inc(d.sem, 16)
            d.cnt += 1
            S.ninstr["sp"] += 1
            tok = ("d", d)
            for b in reads:
                b.readers[("d", id(d))] = tok
            wbuf.last_write = tok

        def copy_any(eng, out_ap, in_ap, reads, writes):
            if eng == "act":
                S.op("act", lambda e: e.copy(out=out_ap, in_=in_ap), reads=reads, writes=writes)
            else:
                S.op(eng, lambda e: e.tensor_copy(out=out_ap, in_=in_ap), reads=reads, writes=writes)

        def stage_win():
            with contextlib.ExitStack() as st:
                mn = alloc_modnorm(st, "wi")
                hT, B_hT = mn[6], mn[7]
                ssw = alloc_stream(st, "wiw")
                qk = [sb(st, f"wi_qk{i}", [128, TT], BF16) for i in range(2)]
                B_qk = [Buf(f"wi_qk{i}") for i in range(2)]
                f32s = [sb(st, f"wi_f{i}", [128, TT], F32) for i in range(2)]
                B_f = [Buf(f"wi_f{i}") for i in range(2)]
                tk = [sb(st, f"wi_tk{i}", [128, 4, 128], F32) for i in range(2)]
                B_tk = [Buf(f"wi_tk{i}") for i in range(2)]
                tkb = [sb(st, f"wi_tkb{i}", [128, 4, 128], BF16) for i in range(2)]
                B_tkb = [Buf(f"wi_tkb{i}") for i in range(2)]
                cn = {"a": 0, "b": 0, "c": 0}
                for tt in TILES:
                    v = 0 if tt < 2 else 1
                    t0 = tt * TT
                    load_modnorm(mn, tt, mp(0, 1, v, 0), mp(0, 1, v, 1))

                    import os as _os
                    WM = int(_os.environ.get("WIN_MODE", "7"))

                    def evac(j, bank, tt=tt, t0=t0):
                        if j < 32 and not (WM & 1):
                            return
                        if 32 <= j < 48 and not (WM & 2):
                            return
                        if j >= 48 and not (WM & 4):
                            return
                        if j < 32:
                            i = cn["a"] % 2
                            cn["a"] += 1
                            S.op("act", lambda e: e.activation(out=qk[i][:], in_=PS[:, bank, :], func=AF.Identity,
                                                               scale=(QSCALE if j < 16 else 1.0)),
                                 reads=[Bps[bank]], writes=[B_qk[i]])
                            dma_nw(QKT[j * 128:(j + 1) * 128, t0:t0 + TT], qk[i][:], [B_qk[i]], B_QKT[tt])
                        if (16 <= j < 32 and tt < 2) or (32 <= j < 48):
                            i = cn["b"] % 2
                            cn["b"] += 1
                            copy_any("dve" if cn["b"] % 2 else "act", f32s[i][:], PS[:, bank, :], [Bps[bank]], [B_f[i]])
                            r0 = 6144 + (j - 16) * 128
                            dma_nw(HT[r0:r0 + 128, t0:t0 + TT], f32s[i][:], [B_f[i]], B_HT[tt])
                        if j >= 48:
                            i = cn["b"] % 2
                            cn["b"] += 1
                            copy_any("act" if cn["b"] % 2 else "dve", f32s[i][:], PS[:, bank, :], [Bps[bank]], [B_f[i]])
                            dma_nw(HT[(j - 48) * 128:(j - 47) * 128, t0:t0 + TT], f32s[i][:], [B_f[i]], B_HT[tt])

                    stream_linear(ssw, lambda j: WIN_d[j], 96, KC, lambda kk: hT[:, kk, :], B_hT, TT, evac, [0, 1, 2, 3])
                S.barrier()

        def stage_kv():
            with contextlib.ExitStack() as st:
                src = [sb(st, f"kv_src{i}", [128, TT], F32) for i in range(2)]
                B_src = [Buf(f"kv_src{i}") for i in range(2)]
                tk = [sb(st, f"kv_tk{i}", [128, 4, 128], F32) for i in range(2)]
                B_tk = [Buf(f"kv_tk{i}") for i in range(2)]
                tkb = [sb(st, f"kv_tkb{i}", [128, 4, 128], BF16) for i in range(2)]
                B_tkb = [Buf(f"kv_tkb{i}") for i in range(2)]
                c = 0
                for kind in range(2):
                    for tt in TILES:
                        if kind == 0 and tt >= 2:
                            continue
                        t0 = tt * TT
                        for h in range(16):
                            i = c % 2
                            bank = c % 2
                            c += 1
                            r0 = 6144 + (kind * 16 + h) * 128
                            S.dma(src[i][:], HT[r0:r0 + 128, t0:t0 + TT], reads=[B_HT[tt]], writes=[B_src[i]])
                            for q in range(4):
                                S.tr(PS[:, bank, q * 128:(q + 1) * 128], src[i][:, q * 128:(q + 1) * 128], ident[:],
                                     reads=[B_src[i], B_const], writes=[Bps[bank]])
                            if tt < 2:
                                S.op("dve", lambda e: e.tensor_copy(out=tk[i][:].rearrange("p a b -> p (a b)"), in_=PS[:, bank, :]),
                                     reads=[Bps[bank]], writes=[B_tk[i]])
                                dstv = nkv if kind == 0 else nvv
                                dma_nw(dstv[:, tt * 4:(tt + 1) * 4, h * 128:(h + 1) * 128], tk[i][:], [B_tk[i]], B_out)
                            if kind == 1:
                                S.op("act", lambda e: e.copy(out=tkb[i][:].rearrange("p a b -> p (a b)"), in_=PS[:, bank, :]),
                                     reads=[Bps[bank]], writes=[B_tkb[i]])
                                dma_nw(VTv[:, tt * 4:(tt + 1) * 4, h * 128:(h + 1) * 128], tkb[i][:], [B_tkb[i]], B_VT[tt])
                S.barrier()

        def softmax_rows(npart, banks, ncol, pb, B_pb, pn, B_pn, mx, B_mx, sm, B_sm):
            nb = len(banks)
            for bi, b in enumerate(banks):
                S.op("dve", lambda e: e.reduce_max(out=mx[:npart, bi:bi + 1], in_=PS[:npart, b, :ncol], axis=AX.X),
                     reads=[Bps[b]], writes=[B_mx])
            if nb > 1:
                S.op("dve", lambda e: e.reduce_max(out=mx[:npart, 2:3], in_=mx[:npart, 0:nb], axis=AX.X),
                     reads=[B_mx], writes=[B_mx])
                S.op("dve", lambda e: e.tensor_scalar(out=mx[:npart, 3:4], in0=mx[:npart, 2:3], scalar1=-1.0, scalar2=None, op0=ALU.mult),
                     reads=[B_mx], writes=[B_mx])
            else:
                S.op("dve", lambda e: e.tensor_scalar(out=mx[:npart, 3:4], in0=mx[:npart, 0:1], scalar1=-1.0, scalar2=None, op0=ALU.mult),
                     reads=[B_mx], writes=[B_mx])
            for bi, b in enumerate(banks):
                S.op("act", lambda e: e.activation(out=pb[:npart, bi * ncol:(bi + 1) * ncol], in_=PS[:npart, b, :ncol], func=AF.Exp,
                                                   bias=mx[:npart, 3:4], scale=1.0, accum_out=sm[:npart, bi:bi + 1]),
                     reads=[Bps[b], B_mx], writes=[B_pb, B_sm])
            if nb > 1:
                S.op("dve", lambda e: e.tensor_tensor(out=sm[:npart, 2:3], in0=sm[:npart, 0:1], in1=sm[:npart, 1:2], op=ALU.add),
                     reads=[B_sm], writes=[B_sm])
                S.op("dve", lambda e: e.reciprocal(out=sm[:npart, 3:4], in_=sm[:npart, 2:3]), reads=[B_sm], writes=[B_sm])
            else:
                S.op("dve", lambda e: e.reciprocal(out=sm[:npart, 3:4], in_=sm[:npart, 0:1]), reads=[B_sm], writes=[B_sm])
            S.op("dve", lambda e: e.tensor_scalar(out=pn[:npart, :nb * ncol], in0=pb[:npart, :nb * ncol], scalar1=sm[:npart, 3:4],
                                                  scalar2=None, op0=ALU.mult),
                 reads=[B_pb, B_sm], writes=[B_pn])

        def stage_attn():
            import os as _os
            with contextlib.ExitStack() as st:
                qa = sb(st, "ap_q", [128, 16, 256], BF16)
                ka = sb(st, "ap_k", [128, 16, 256], BF16)
                va = sb(st, "ap_v", [128, 2, 2048], BF16)
                oa = sb(st, "ap_o", [128, 16, 256], BF16)
                B_qa, B_ka, B_va, B_oa = Buf("ap_q"), Buf("ap_k"), Buf("ap_v"), Buf("ap_o")
                pb = [sb(st, f"ap_pb{i}", [128, 256], F32) for i in range(2)]
                pn = [sb(st, f"ap_pn{i}", [128, 256], BF16) for i in range(2)]
                mx = [sb(st, f"ap_mx{i}", [128, 4], F32) for i in range(2)]
                sm = [sb(st, f"ap_sm{i}", [128, 4], F32) for i in range(2)]
                B_pb = [Buf(f"ap_pb{i}") for i in range(2)]
                B_pn = [Buf(f"ap_pn{i}") for i in range(2)]
                B_mx = [Buf(f"ap_mx{i}") for i in range(2)]
                B_sm = [Buf(f"ap_sm{i}") for i in range(2)]
                pT = [sb(st, f"ap_pT{i}", [128, 4, 128], BF16) for i in range(2)]
                B_pT = [Buf(f"ap_pT{i}") for i in range(2)]
                for s in (range(4) if 'p' in _os.environ.get('ATTN_PART', 'ps') else []):
                    t0 = s * 256
                    tt = s // 2
                    S.dma(qa[:], QKv[:, 0:16, t0:t0 + 256], reads=[B_QKT[tt]], writes=[B_qa])
                    S.dma(ka[:], QKv[:, 16:32, t0:t0 + 256], reads=[B_QKT[tt]], writes=[B_ka])
                    S.dma(va[:], VTv[:, 2 * s:2 * s + 2, :], reads=[B_VT[tt]], writes=[B_va])
                    for h in range(16):
                        j = h % 2
                        for qb in range(2):
                            i = qb
                            bank = qb
                            S.mm(PS[:, bank, :256], [(qa[:, h, qb * 128:(qb + 1) * 128], ka[:, h, :])],
                                 reads=[B_qa, B_ka], writes=[Bps[bank]])
                            softmax_rows(128, [bank], 256, pb[i], B_pb[i], pn[i], B_pn[i], mx[i], B_mx[i], sm[i], B_sm[i])
                            for kb in range(2):
                                S.tr(PSB[:, (qb * 2 + kb) * 128:(qb * 2 + kb + 1) * 128], pn[i][:, kb * 128:(kb + 1) * 128], identb[:],
                                     reads=[B_pn[i], B_const], writes=[BpsB])
                        S.op("act", lambda e: e.copy(out=pT[j][:].rearrange("p a b -> p (a b)"), in_=PSB[:, 0:512]),
                             reads=[BpsB], writes=[B_pT[j]])
                        for qb in range(2):
                            S.mm(PS[:, 2 + j, qb * 128:(qb + 1) * 128],
                                 [(va[:, kb, h * 128:(h + 1) * 128], pT[j][:, qb * 2 + kb, :]) for kb in range(2)],
                                 reads=[B_va, B_pT[j]], writes=[Bps[2 + j]])
                        S.op("dve", lambda e: e.tensor_copy(out=oa[:, h, :], in_=PS[:, 2 + j, :256]),
                             reads=[Bps[2 + j]], writes=[B_oa])
                    dma_nw(MIXv[:, 0:16, t0:t0 + 256], oa[:], [B_oa], B_MIXT[tt])
                S.barrier()
            with contextlib.ExitStack() as st:
                tzf = sb(st, "as_tzf", [64, 16, 15, 64], F32)
                tzb = sb(st, "as_tzb", [64, 16, 15, 64], BF16)
                mk = sb(st, "as_mk", [64, 64], F32)
                B_tzf, B_tzb, B_mk = Buf("tzf"), Buf("tzb"), Buf("mk")
                S.dma(mk[:], cst["maskc"], writes=[B_mk])
                S.dma(tzf[:].rearrange("p a b c -> p (a b c)"), tzh_d, writes=[B_tzf])
                for h in range(16):
                    S.op("dve", lambda e: e.tensor_tensor(out=tzb[:, h, :, :], in0=tzf[:, h, :, :],
                                                          in1=mk[:, :].unsqueeze(1).broadcast_to([64, 15, 64]), op=ALU.add),
                         reads=[B_tzf, B_mk], writes=[B_tzb])
                NS = 2
                qh = [sb(st, f"as_q{i}", [128, 1024], BF16) for i in range(NS)]
                kh = [sb(st, f"as_k{i}", [128, 1024], BF16) for i in range(NS)]
                kc = [sb(st, f"as_kc{i}", [128, 512], BF16) for i in range(NS)]
                ckf = [sb(st, f"as_ckf{i}", [128, 4, 128], F32) for i in range(NS)]
                ckb = [sb(st, f"as_ckb{i}", [128, 4, 128], BF16) for i in range(NS)]
                Bckb = [Buf(f"as_ckb{i}") for i in range(NS)]
                cvf = [sb(st, f"as_cvf{i}", [128, 4, 128], F32) for i in range(NS)]
                vc = [sb(st, f"as_vc{i}", [128, 4, 128], BF16) for i in range(NS)]
                ve = [sb(st, f"as_ve{i}", [128, 8, 128], BF16) for i in range(NS)]
                vo = [sb(st, f"as_vo{i}", [128, 7, 128], BF16) for i in range(NS)]
                oh = [sb(st, f"as_o{i}", [128, 1024], BF16) for i in range(NS)]
                Bq = [Buf(f"as_q{i}") for i in range(NS)]
                Bk = [Buf(f"as_k{i}") for i in range(NS)]
                Bkc = [Buf(f"as_kc{i}") for i in range(NS)]
                Bckf = [Buf(f"as_ckf{i}") for i in range(NS)]
                Bcvf = [Buf(f"as_cvf{i}") for i in range(NS)]
                Bvc = [Buf(f"as_vc{i}") for i in range(NS)]
                Bve = [Buf(f"as_ve{i}") for i in range(NS)]
                Bvo = [Buf(f"as_vo{i}") for i in range(NS)]
                Boh = [Buf(f"as_o{i}") for i in range(NS)]
                pb = [sb(st, f"as_pb{i}", [64, 1024], F32) for i in range(2)]
                pn = [sb(st, f"as_pn{i}", [64, 1024], BF16) for i in range(2)]
                mx = [sb(st, f"as_mx{i}", [64, 4], F32) for i in range(2)]
                sm = [sb(st, f"as_sm{i}", [64, 4], F32) for i in range(2)]
                pT = [sb(st, f"as_pT{i}", [128, 8, 64], BF16) for i in range(2)]
                B_pb = [Buf(f"as_pb{i}") for i in range(2)]
                B_pn = [Buf(f"as_pn{i}") for i in range(2)]
                B_mx = [Buf(f"as_mx{i}") for i in range(2)]
                B_sm = [Buf(f"as_sm{i}") for i in range(2)]
                B_pT = [Buf(f"as_pT{i}") for i in range(2)]
                ckv = ck_d.rearrange("(b p) c -> p b c", p=128)
                cvv = cv_d.rearrange("(b p) c -> p b c", p=128)
                VTo = VT[1024 + 64:1024 + 64 + 7 * 128, :].rearrange("(b p) c -> p b c", p=128)
                BVs = [B_VT[2], B_VT[3]]
                BQs = [B_QKT[2], B_QKT[3]]
                it = 0
                for h in (range(16) if 's' in _os.environ.get('ATTN_PART', 'ps') else []):
                    n = h % NS
                    hs = slice(h * 128, (h + 1) * 128)
                    S.dma(qh[n][:], QKv[:, h, 1024:2048], reads=BQs, writes=[Bq[n]])
                    S.dma(kh[n][:], QKv[:, 16 + h, 1024:2048], reads=BQs, writes=[Bk[n]])
                    S.dma(ve[n][:], VTv[:, 8:16, hs], reads=BVs, writes=[Bve[n]])
                    S.dma(vo[n][:], VTo[:, :, hs], reads=BVs, writes=[Bvo[n]])
                    S.dma(ckf[n][:], ckv[:, :, hs], writes=[Bckf[n]])
                    S.dma(cvf[n][:], cvv[:, :, hs], writes=[Bcvf[n]])
                    S.op("pool", lambda e: e.tensor_copy(out=vc[n][:], in_=cvf[n][:]), reads=[Bcvf[n]], writes=[Bvc[n]])
                    S.op("pool", lambda e: e.tensor_copy(out=ckb[n][:], in_=ckf[n][:]), reads=[Bckf[n]], writes=[Bckb[n]])
                    for kb in range(4):
                        S.tr(PSB[:, kb * 128:(kb + 1) * 128], ckb[n][:, kb, :], identb[:], reads=[Bckb[n], B_const], writes=[BpsB])
                    S.op("act", lambda e: e.copy(out=kc[n][:], in_=PSB[:, 0:512]), reads=[BpsB], writes=[Bkc[n]])
                    for r in range(16):
                        i = it % 2
                        it += 1
                        start = min(max(r - 4, 0), 8)
                        dr0 = start - r + 7
                        qsl = qh[n][:, r * 64:(r + 1) * 64]
                        S.mm(PS[:64, 0, :], [(qsl, kh[n][:, start * 64:start * 64 + 512]),
                                             (identb[:64, :64], tzb[:, h, dr0:dr0 + 8, :].rearrange("p a b -> p (a b)"))],
                             reads=[Bq[n], Bk[n], B_tzb, B_const], writes=[Bps[0]])
                        S.mm(PS[:64, 1, :], [(qsl, kc[n][:, :])], reads=[Bq[n], Bkc[n]], writes=[Bps[1]])
                        softmax_rows(64, [0, 1], 512, pb[i], B_pb[i], pn[i], B_pn[i], mx[i], B_mx[i], sm[i], B_sm[i])
                        for c8 in range(8):
                            S.tr(PSB[:, c8 * 64:(c8 + 1) * 64], pn[i][:, c8 * 128:(c8 + 1) * 128], identb[:64, :64],
                                 reads=[B_pn[i], B_const], writes=[BpsB])
                        S.op("act", lambda e: e.copy(out=pT[i][:].rearrange("p a b -> p (a b)"), in_=PSB[:, 0:512]),
                             reads=[BpsB], writes=[B_pT[i]])
                        pairs = []
                        for c8 in range(4):
                            if start % 2 == 0:
                                vb = ve[n][:, start // 2 + c8, :]
                            else:
                                vb = vo[n][:, (start - 1) // 2 + c8, :]
                            pairs.append((vb, pT[i][:, c8, :]))
                        for c8 in range(4):
                            pairs.append((vc[n][:, c8, :], pT[i][:, 4 + c8, :]))
                        ob = 2 + (r // 8)
                        S.mm(PS[:, ob, (r % 8) * 64:(r % 8 + 1) * 64], pairs, reads=[Bve[n], Bvo[n], Bvc[n], B_pT[i]], writes=[Bps[ob]])
                        if r % 8 == 7:
                            S.op("dve", lambda e: e.tensor_copy(out=oh[n][:, (r // 8) * 512:(r // 8 + 1) * 512], in_=PS[:, ob, :]),
                                 reads=[Bps[ob]], writes=[Boh[n]])
                    dma_nw(MIXv[:, h, 1024:2048], oh[n][:], [Boh[n]], B_MIXT[2])
                    B_MIXT[3].last_write = B_MIXT[2].last_write
                S.barrier()

        def stage_hyena():
            C = 256
            NCG = 2048 // C
            TWO_PI = float(2 * math.pi)
            with contextlib.ExitStack() as st:
                dl = sb(st, "hy_dl", [128, 2048], F32)
                wsT = sb(st, "hy_ws", [128, 144], F32)
                bsT = sb(st, "hy_bs", [128, 48], F32)
                sgn = sb(st, "hy_sgn", [128, 1], F32)
                fw1 = sb(st, "hy_fw1", [33, 64], F32)
                fw2 = sb(st, "hy_fw2", [64, 64], F32)
                fq = sb(st, "hy_fq", [64, 2], F32)
                fb = sb(st, "hy_fb", [64, 2], F32)
                fa = sb(st, "hy_fa", [64, 2], F32)
                B_hc = Buf("hy_const")
                S.dma(dl[:], cst["deltas"].partition_broadcast(128), writes=[B_hc])
                S.dma(wsT[:], wsT_d, writes=[B_hc])
                S.dma(bsT[:], bsT_d, writes=[B_hc])
                S.dma(sgn[:], cst["sgn"], writes=[B_hc])
                S.dma(fw1[:], fw1_d, writes=[B_hc])
                S.dma(fw2[:], fw2_d, writes=[B_hc])
                S.dma(fq[:], ffq_d, writes=[B_hc])
                S.dma(fb[:, 0:1], fb1_d, writes=[B_hc])
                S.dma(fb[:, 1:2], fb2_d, writes=[B_hc])
                S.op("dve", lambda e: e.tensor_scalar(out=fa[:], in0=fq[:], scalar1=1.0 / TWO_PI, scalar2=None, op0=ALU.mult),
                     reads=[B_hc], writes=[B_hc])
                S.op("dve", lambda e: e.tensor_tensor(out=fb[:], in0=fb[:], in1=fa[:], op=ALU.mult), reads=[B_hc], writes=[B_hc])
                tmpu = sb(st, "hy_tmpu", [64, 512], F32)
                B_tmpu = Buf("hy_tmpu")

                def sin_layer(out_ap, ps_ap, li):
                    S.op("dve", lambda e: e.tensor_scalar(out=tmpu[:], in0=ps_ap, scalar1=fa[:, li:li + 1], scalar2=fb[:, li:li + 1],
                                                          op0=ALU.mult, op1=ALU.add), reads=[Bps[0], B_hc], writes=[B_tmpu])
                    for (thr, cmp, op1) in ((-0.5, ALU.is_lt, ALU.add), (-0.5, ALU.is_lt, ALU.add), (0.5, ALU.is_gt, ALU.subtract),
                                            (-0.5, ALU.is_lt, ALU.add)):
                        S.op("dve", lambda e: e.scalar_tensor_tensor(out=tmpu[:], in0=tmpu[:], scalar=thr, in1=tmpu[:], op0=cmp, op1=op1),
                             reads=[B_tmpu], writes=[B_tmpu])
                    S.op("act", lambda e: e.activation(out=out_ap, in_=tmpu[:], func=AF.Sin, scale=-TWO_PI), reads=[B_tmpu], writes=[B_h12])

                B_h12 = Buf("hy_h12")
                for (l, nseq, tok0) in ((256, 4, 0), (1024, 1, 1024)):
                    NT2 = 2 * l // 128
                    NH = l // 128
                    with contextlib.ExitStack() as sl:
                        h2b = sb(sl, f"hy_h2b{l}", [64, 2 * l], BF16)
                        B_h2b = B_h12
                        ntx = sb(sl, f"hy_ntx{l}", [128, NT2], F32)
                        FTs = sb(sl, f"hy_FT{l}", [128, NH, 2 * l], BF16)
                        GIs = sb(sl, f"hy_GI{l}", [128, NT2, l], BF16)
                        B_lc = Buf(f"hy_lc{l}")
                        S.dma(ntx[:], cst[f"ntx{l}"], writes=[B_lc])
                        S.dma(FTs[:], cst[f"FT{l}"], writes=[B_lc])
                        S.dma(GIs[:], cst[f"GI{l}"], writes=[B_lc])
                        with contextlib.ExitStack() as sm_:
                            zT = sb(sm_, f"hy_zT{l}", [33, 2 * l], F32)
                            h1T = sb(sm_, f"hy_h1T{l}", [64, 2 * l], F32)
                            S.dma(zT[:], cst[f"zT{l}"], writes=[B_lc])
                            for n0 in range(0, 2 * l, 512):
                                S.mm(PS[:64, 0, :], [(fw1[:, :], zT[:, n0:n0 + 512])], reads=[B_hc, B_lc], writes=[Bps[0]])
                                sin_layer(h1T[:, n0:n0 + 512], PS[:64, 0, :], 0)
                            for n0 in range(0, 2 * l, 512):
                                S.mm(PS[:64, 0, :], [(fw2[:, :], h1T[:, n0:n0 + 512])], reads=[B_hc, B_h12], writes=[Bps[0]])
                                sin_layer(h2b[:, n0:n0 + 512], PS[:64, 0, :], 1)
                            S.barrier()
                        w3s = [sb(sl, f"hy_w3_{l}_{i}", [64, C], F32) for i in range(2)]
                        B_w3 = [Buf(f"hy_w3_{i}") for i in range(2)]
                        w3b = [sb(sl, f"hy_w3b_{l}_{i}", [64, C], BF16) for i in range(2)]
                        B_w3b = [Buf(f"hy_w3b_{i}") for i in range(2)]
                        usb = [sb(sl, f"hy_usb{l}_{i}", [128, l], BF16) for i in range(2)]
                        B_usb = [Buf(f"hy_usb{i}") for i in range(2)]
                        dec = [sb(sl, f"hy_dec{l}_{i}", [128, C], F32) for i in range(2)]
                        B_dec = [Buf(f"hy_dec{i}") for i in range(2)]
                        filt = sb(sl, f"hy_filt{l}", [128, NT2, C], BF16)
                        B_filt = Buf("hy_filt")
                        sq = [sb(sl, f"hy_sq{l}_{i}", [128, C], BF16) for i in range(2)]
                        B_sq = [Buf(f"hy_sq{i}") for i in range(2)]
                        rn = sb(sl, f"hy_rn{l}", [128, C], F32)
                        B_rn = Buf("hy_rn")
                        bbc = sb(sl, f"hy_bbc{l}", [128, 2, C], F32)
                        B_bbc = Buf("hy_bbc")
                        Kf = sb(sl, f"hy_Kf{l}", [128, 2, NT2, C], F32)
                        B_Kf = Buf("hy_Kf")
                        knyq = sb(sl, f"hy_knyq{l}", [1, 2, C], F32)
                        B_kn = Buf("hy_knyq")
                        tA = [sb(sl, f"hy_tA{l}_{i}", [128, C], F32) for i in range(2)]
                        B_tA = [Buf(f"hy_tA{i}") for i in range(2)]
                        uin = [sb(sl, f"hy_u{l}_{i}", [128, l], F32) for i in range(2)]
                        B_uin = [Buf(f"hy_u{i}") for i in range(2)]
                        scv = [sb(sl, f"hy_sc{l}_{i}", [128, l], F32) for i in range(2)]
                        B_scv = [Buf(f"hy_sc{i}") for i in range(2)]
                        x2s = sb(sl, f"hy_x2s{l}", [128, 2, l], F32)
                        B_x2s = Buf("hy_x2s")
                        vtm = sb(sl, f"hy_vtm{l}", [128, NH, C], BF16)
                        B_vtm = Buf("hy_vtm")
                        x1tm = sb(sl, f"hy_x1tm{l}", [128, NH, C], F32)
                        B_x1tm = Buf("hy_x1tm")
                        z1tm = sb(sl, f"hy_z1tm{l}", [128, NH, C], BF16)
                        B_z1tm = Buf("hy_z1tm")
                        Yf = sb(sl, f"hy_Yf{l}", [128, NT2, C], BF16)
                        B_Yf = Buf("hy_Yf")
                        urs = [sb(sl, f"hy_ur{l}_{i}", [128, C], F32) for i in range(2)]
                        uis = [sb(sl, f"hy_ui{l}_{i}", [128, C], F32) for i in range(2)]
                        B_urs = [Buf(f"hy_ur{i}") for i in range(2)]
                        B_uis = [Buf(f"hy_ui{i}") for i in range(2)]
                        t4 = [sb(sl, f"hy_t{l}_{i}", [128, C], F32) for i in range(4)]
                        B_t4 = [Buf(f"hy_t{i}") for i in range(4)]
                        zo = [sb(sl, f"hy_zo{l}_{i}", [128, 512], BF16) for i in range(2)]
                        B_zo = [Buf(f"hy_zo{i}") for i in range(2)]
                        cnt = {"w3": 0, "u": 0, "p": 0, "zo": 0, "bk": 0}

                        def freq_mult(src_tm, B_src, o):
                            for p in range(NH):
                                S.mm(PS[:, 0, :C], [(FTs[:, tb, p * 128:(p + 1) * 128], src_tm[:, tb, :]) for tb in range(NH)],
                                     reads=[B_lc, B_src], writes=[Bps[0]])
                                S.mm(PS[:, 1, :C], [(FTs[:, tb, (NH + p) * 128:(NH + p + 1) * 128], src_tm[:, tb, :]) for tb in range(NH)],
                                     reads=[B_lc, B_src], writes=[Bps[1]])
                                i = cnt["p"] % 2
                                cnt["p"] += 1
                                S.op("act", lambda e: e.copy(out=urs[i][:], in_=PS[:, 0, :C]), reads=[Bps[0]], writes=[B_urs[i]])
                                S.op("act", lambda e: e.copy(out=uis[i][:], in_=PS[:, 1, :C]), reads=[Bps[1]], writes=[B_uis[i]])
                                Kr = Kf[:, o, p, :]
                                Ki = Kf[:, o, NH + p, :]
                                S.op("dve", lambda e: e.tensor_tensor(out=t4[0][:], in0=urs[i][:], in1=Kr, op=ALU.mult),
                                     reads=[B_urs[i], B_Kf], writes=[B_t4[0]])
                                S.op("pool", lambda e: e.tensor_tensor(out=t4[1][:], in0=uis[i][:], in1=Ki, op=ALU.mult),
                                     reads=[B_uis[i], B_Kf], writes=[B_t4[1]])
                                S.op("dve", lambda e: e.tensor_tensor(out=Yf[:, p, :], in0=t4[0][:], in1=t4[1][:], op=ALU.subtract),
                                     reads=[B_t4[0], B_t4[1]], writes=[B_Yf])
                                S.op("pool", lambda e: e.tensor_tensor(out=t4[2][:], in0=urs[i][:], in1=Ki, op=ALU.mult),
                                     reads=[B_urs[i], B_Kf], writes=[B_t4[2]])
                                S.op("dve", lambda e: e.tensor_tensor(out=t4[3][:], in0=uis[i][:], in1=Kr, op=ALU.mult),
                                     reads=[B_uis[i], B_Kf], writes=[B_t4[3]])
                                S.op("dve", lambda e: e.tensor_tensor(out=Yf[:, NH + p, :], in0=t4[2][:], in1=t4[3][:], op=ALU.add),
                                     reads=[B_t4[2], B_t4[3]], writes=[B_Yf])
                                if p == 0:
                                    S.op("dve", lambda e: e.tensor_tensor(out=Yf[0:1, NH, :], in0=uis[i][0:1, :], in1=knyq[0:1, o, :], op=ALU.mult),
                                         reads=[B_uis[i], B_kn, B_Yf], writes=[B_Yf])

                        for cg in range(NCG):
                            csl = slice(cg * C, (cg + 1) * C)
                            S.dma(bbc[:].rearrange("p a b -> p (a b)").rearrange("p (a b) -> p a b", a=2),
                                  fbias_d[:, csl].unsqueeze(0).broadcast_to([128, 2, C]), writes=[B_bbc])
                            for o in range(2):
                                for tq in range(NT2):
                                    dr = 0 if tq < NH else 1
                                    if tq == 0 or tq == NH:
                                        wi = cnt["w3"] % 2
                                        cnt["w3"] += 1
                                        c0 = o * 4096 + dr * 2048 + cg * C
                                        S.dma(w3s[wi][:], fw3_d[:, c0:c0 + C], writes=[B_w3[wi]])
                                        S.op("pool", lambda e: e.tensor_copy(out=w3b[wi][:], in_=w3s[wi][:]), reads=[B_w3[wi]], writes=[B_w3b[wi]])
                                    bk = 2 + (cnt["bk"] % 2)
                                    cnt["bk"] += 1
                                    S.mm(PS[:, bk, :C], [(h2b[:, tq * 128:(tq + 1) * 128], w3b[wi][:, :])], reads=[B_h2b, B_w3b[wi]], writes=[Bps[bk]])
                                    di = tq % 2
                                    S.op("act", lambda e: e.activation(out=dec[di][:], in_=dl[:, csl], func=AF.Exp, scale=ntx[:, tq:tq + 1]),
                                         reads=[B_hc, B_lc], writes=[B_dec[di]])
                                    S.op("dve", lambda e: e.tensor_tensor(out=filt[:, tq, :], in0=PS[:, bk, :C], in1=dec[di][:], op=ALU.mult),
                                         reads=[Bps[bk], B_dec[di]], writes=[B_filt])
                                    S.op("pool", lambda e: e.tensor_tensor(out=sq[di][:], in0=filt[:, tq, :], in1=filt[:, tq, :], op=ALU.mult),
                                         reads=[B_filt], writes=[B_sq[di]])
                                    S.mm(PS[:, 6, :C], [(onesb[:], sq[di][:])], reads=[B_sq[di], B_const], writes=[Bps[6]],
                                         start=(tq == 0), stop=(tq == NT2 - 1))
                                S.op("act", lambda e: e.activation(out=rn[:], in_=PS[:, 6, :C], func=AF.Sqrt, bias=EPS, scale=1.0),
                                     reads=[Bps[6]], writes=[B_rn])
                                S.op("dve", lambda e: e.reciprocal(out=rn[:], in_=rn[:]), reads=[B_rn], writes=[B_rn])
                                for ft in range(NT2):
                                    S.mm(PS[:, 0, :C], [(FTs[:, tb, ft * 128:(ft + 1) * 128], filt[:, tb, :]) for tb in range(NH)],
                                         reads=[B_lc, B_filt], writes=[Bps[0]])
                                    S.mm(PS[:, 1, :C], [(FTs[:, tb, ft * 128:(ft + 1) * 128], filt[:, NH + tb, :]) for tb in range(NH)],
                                         reads=[B_lc, B_filt], writes=[Bps[1]])
                                    ai = ft % 2
                                    S.op("act", lambda e: e.copy(out=tA[ai][:], in_=PS[:, 0, :C]), reads=[Bps[0]], writes=[B_tA[ai]])
                                    S.op("dve", lambda e: e.scalar_tensor_tensor(out=tA[ai][:], in0=PS[:, 1, :C], scalar=sgn[:, 0:1], in1=tA[ai][:],
                                                                                 op0=ALU.mult, op1=ALU.add),
                                         reads=[Bps[1], B_tA[ai], B_hc], writes=[B_tA[ai]])
                                    S.op("dve", lambda e: e.tensor_tensor(out=Kf[:, o, ft, :], in0=tA[ai][:], in1=rn[:], op=ALU.mult),
                                         reads=[B_tA[ai], B_rn], writes=[B_Kf])
                                    if ft < NH:
                                        S.op("pool", lambda e: e.tensor_tensor(out=Kf[:, o, ft, :], in0=Kf[:, o, ft, :], in1=bbc[:, o, :], op=ALU.add),
                                             reads=[B_Kf, B_bbc], writes=[B_Kf])
                                    if ft == NH:
                                        S.op("dve", lambda e: e.tensor_tensor(out=knyq[0:1, o, :], in0=Kf[0:1, o, NH, :], in1=bbc[0:1, o, :], op=ALU.add),
                                             reads=[B_Kf, B_bbc], writes=[B_kn])
                                        S.op("dve", lambda e: e.memset(Kf[0:1, o, NH, :], 0.0), reads=[B_kn], writes=[B_Kf])
                            for s in range(nseq):
                                ts0 = tok0 + s * l
                                tts = sorted(set([ts0 // TT, (ts0 + l - 1) // TT]))
                                rB = [B_HT[t] for t in tts]
                                for part in range(3):
                                    for ct in range(2):
                                        jrow = part * 16 + cg * 2 + ct
                                        ui = cnt["u"] % 2
                                        cnt["u"] += 1
                                        S.dma(uin[ui][:], HTv[:, jrow, ts0:ts0 + l], reads=rB, writes=[B_uin[ui]])
                                        if part == 2:
                                            dst, Bdst = x2s[:, ct, :], B_x2s
                                        else:
                                            dst, Bdst = scv[ui][:], B_scv[ui]
                                        u = uin[ui]
                                        S.op("dve", lambda e: e.tensor_scalar(out=dst, in0=u[:], scalar1=wsT[:, jrow * 3 + 1:jrow * 3 + 2],
                                                                              scalar2=bsT[:, jrow:jrow + 1], op0=ALU.mult, op1=ALU.add),
                                             reads=[B_uin[ui], B_hc], writes=[Bdst])
                                        S.op("dve", lambda e: e.scalar_tensor_tensor(out=dst[:, 1:l], in0=u[:, 0:l - 1], scalar=wsT[:, jrow * 3:jrow * 3 + 1],
                                                                                     in1=dst[:, 1:l], op0=ALU.mult, op1=ALU.add),
                                             reads=[B_uin[ui], B_hc, Bdst], writes=[Bdst])
                                        S.op("dve", lambda e: e.scalar_tensor_tensor(out=dst[:, 0:l - 1], in0=u[:, 1:l], scalar=wsT[:, jrow * 3 + 2:jrow * 3 + 3],
                                                                                     in1=dst[:, 0:l - 1], op0=ALU.mult, op1=ALU.add),
                                             reads=[B_uin[ui], B_hc, Bdst], writes=[Bdst])
                                        if part < 2:
                                            S.op("act", lambda e: e.copy(out=usb[ui][:], in_=dst), reads=[Bdst], writes=[B_usb[ui]])
                                            for q in range(NH):
                                                S.tr(PSB[:, q * 128:(q + 1) * 128], usb[ui][:, q * 128:(q + 1) * 128], identb[:],
                                                     reads=[B_usb[ui], B_const], writes=[BpsB])
                                            tgt, Btgt = (vtm, B_vtm) if part == 0 else (x1tm, B_x1tm)
                                            S.op("dve", lambda e: e.tensor_copy(out=tgt[:, 0:NH, ct * 128:(ct + 1) * 128],
                                                                                in_=PSB[:, :NH * 128].rearrange("p (a b) -> p a b", a=NH)),
                                                 reads=[BpsB], writes=[Btgt])
                                freq_mult(vtm, B_vtm, 0)
                                for tb in range(NH):
                                    bk = 2 + (tb % 2)
                                    S.mm(PS[:, bk, :C], [(GIs[:, ft, tb * 128:(tb + 1) * 128], Yf[:, ft, :]) for ft in range(NT2)],
                                         reads=[B_lc, B_Yf], writes=[Bps[bk]])
                                    S.op("dve", lambda e: e.tensor_tensor(out=z1tm[:, tb, :], in0=x1tm[:, tb, :], in1=PS[:, bk, :C], op=ALU.mult),
                                         reads=[Bps[bk], B_x1tm], writes=[B_z1tm])
                                freq_mult(z1tm, B_z1tm, 1)
                                for ct in range(2):
                                    for n0 in range(0, l, 512):
                                        nn = min(512, l - n0)
                                        bk = 2 + (cnt["zo"] % 2)
                                        zi = cnt["zo"] % 2
                                        cnt["zo"] += 1
                                        S.mm(PS[:, bk, :nn], [(Yf[:, ft, ct * 128:(ct + 1) * 128], GIs[:, ft, n0:n0 + nn]) for ft in range(NT2)],
                                             reads=[B_lc, B_Yf], writes=[Bps[bk]])
                                        S.op("dve", lambda e: e.tensor_tensor(out=zo[zi][:, :nn], in0=x2s[:, ct, n0:n0 + nn], in1=PS[:, bk, :nn], op=ALU.mult),
                                             reads=[Bps[bk], B_x2s], writes=[B_zo[zi]])
                                        row = 2048 + cg * C + ct * 128
                                        dma_nw(MIXT[row:row + 128, ts0 + n0:ts0 + n0 + nn], zo[zi][:, :nn], [B_zo[zi]], B_MIXT[(ts0 + n0) // TT])
                        S.barrier()
                S.barrier()

        def stage_proj_resid(tag, Wd, gate_l, src_bufs):
            with contextlib.ExitStack() as st:
                hT = sb(st, f"{tag}_hT", [128, KC, TT], BF16)
                B_hT = Buf(f"{tag}_hT")
                xs4 = [sb(st, f"{tag}_xs{i}", [128, TT], F32) for i in range(4)]
                B_xs4 = [Buf(f"{tag}_xs{i}") for i in range(4)]
                ost = [sb(st, f"{tag}_ost{q}", [128, TT], F32) for q in range(2)]
                B_ost = [Buf(f"{tag}_ost{q}") for q in range(2)]
                ssw = alloc_stream(st, f"{tag}w")
                for tt in TILES:
                    v = 0 if tt < 2 else 1
                    t0 = tt * TT
                    S.dma(hT[:], MIXv[:, :, t0:t0 + TT], reads=[src_bufs[tt]], writes=[B_hT])
                    ev = resid_evac((xs4, B_xs4), ost, B_ost, tt, mp(gate_l, 1, v, 2))
                    stream_linear(ssw, lambda m: Wd[m], KC, KC, lambda kk: hT[:, kk, :], [B_hT], TT, ev, [0, 1, 2, 3])
                    ev.flush()
                S.barrier()

        def stage_fnet():
            l = 1
            FAv = FA.rearrange("(b p) g c -> p b g c", p=128)
            with contextlib.ExitStack() as st:
                mn = alloc_modnorm(st, "fn")
                hT, B_hT = mn[6], mn[7]
                csb = sb(st, "fn_cs", [128, 4, 1024], BF16)
                B_cs = Buf("fn_cs")
                S.dma(csb[:], cst["CS"], writes=[B_cs])
                fa = [sb(st, f"fn_fa{i}", [128, 1024], BF16) for i in range(2)]
                B_fa = [Buf(f"fn_fa{i}") for i in range(2)]
                c = 0
                for tt in TILES:
                    v = 0 if tt < 2 else 1
                    load_modnorm(mn, tt, mp(l, 1, v, 0), mp(l, 1, v, 1))
                    for tb in range(4):
                        for g in range(8):
                            for half in range(2):
                                bank = half
                                S.mm(PS[:, bank, :], [(hT[:, g * 4 + ck, tb * 128:(tb + 1) * 128], csb[:, ck, half * 512:(half + 1) * 512])
                                                      for ck in range(4)],
                                     reads=B_hT[g * 4:g * 4 + 4] + [B_cs], writes=[Bps[bank]])
                            i = c % 2
                            c += 1
                            S.op("act", lambda e: e.copy(out=fa[i][:, 0:512], in_=PS[:, 0, :]), reads=[Bps[0]], writes=[B_fa[i]])
                            S.op("dve", lambda e: e.tensor_copy(out=fa[i][:, 512:1024], in_=PS[:, 1, :]), reads=[Bps[1]], writes=[B_fa[i]])
                            dma_nw(FAv[:, tt * 4 + tb, g, :], fa[i][:], [B_fa[i]], B_FA[tt])
                S.barrier()
            with contextlib.ExitStack() as st:
                cl = {}
                B_cl = Buf("fn_cl")
                for L in (256, 1024):
                    cl[L] = (sb(st, f"fn_cl{L}", [128, L // 128, L], BF16), sb(st, f"fn_sl{L}", [128, L // 128, L], BF16))
                    S.dma(cl[L][0][:], cst[f"CL{L}"], writes=[B_cl])
                    S.dma(cl[L][1][:], cst[f"SL{L}"], writes=[B_cl])
                fin = [sb(st, f"fn_in{i}", [128, 8, 1024], BF16) for i in range(2)]
                B_fin = [Buf(f"fn_in{i}") for i in range(2)]
                mo = [sb(st, f"fn_mo{i}", [128, 512], BF16) for i in range(2)]
                B_mo = [Buf(f"fn_mo{i}") for i in range(2)]
                c = 0
                c2 = 0
                seqs = [(s * 256, 256) for s in range(4)] + [(1024, 1024)]
                for (tok0, L) in seqs:
                    nb = L // 128
                    tts = sorted(set([tok0 // TT, (tok0 + L - 1) // TT]))
                    for g in range(8):
                        i = c % 2
                        c += 1
                        S.dma(fin[i][:, :nb, :], FAv[:, tok0 // 128:tok0 // 128 + nb, g, :], reads=[B_FA[t] for t in tts], writes=[B_fin[i]])
                        for kt in range(4):
                            for n0 in range(0, L, 512):
                                nn = min(512, L - n0)
                                bank = c2 % 4
                                o = c2 % 2
                                c2 += 1
                                pairs = []
                                for tb in range(nb):
                                    pairs.append((fin[i][:, tb, kt * 128:(kt + 1) * 128], cl[L][0][:, tb, n0:n0 + nn]))
                                    pairs.append((fin[i][:, tb, 512 + kt * 128:512 + (kt + 1) * 128], cl[L][1][:, tb, n0:n0 + nn]))
                                S.mm(PS[:, bank, :nn], pairs, reads=[B_fin[i], B_cl], writes=[Bps[bank]])
                                copy_any("act" if o else "dve", mo[o][:, :nn], PS[:, bank, :nn], [Bps[bank]], [B_mo[o]])
                                tq = (tok0 + n0) // TT
                                dma_nw(MIXT[(g * 4 + kt) * 128:(g * 4 + kt + 1) * 128, tok0 + n0:tok0 + n0 + nn], mo[o][:, :nn], [B_mo[o]], B_MIXT[tq])
                S.barrier()

        def stage_final():
            with contextlib.ExitStack() as st:
                mn = alloc_modnorm(st, "fin", with_h=False)
                (xs4, B_xs4, sq2, B_sq2, rstd, B_rstd, _, _) = mn
                yT = [sb(st, f"fin_y{i}", [128, TT], F32) for i in range(2)]
                B_yT = [Buf(f"fin_y{i}") for i in range(2)]
                orow = [sb(st, f"fin_o{i}", [128, D], F32) for i in range(4)]
                B_orow = [Buf(f"fin_o{i}") for i in range(4)]
                fg = gT[:, 6 * 32:7 * 32]
                for tt in TILES:
                    t0 = tt * TT
                    load_modnorm(mn, tt, None, None, f32stats=True)
                    for k in range(KC):
                        i = k % 4
                        y = k % 2
                        S.dma(xs4[i][:], XTv[k, :, t0:t0 + TT], reads=[B_XT[tt][k]], writes=[B_xs4[i]])
                        S.op("dve", lambda e: e.scalar_tensor_tensor(out=yT[y][:], in0=xs4[i][:], scalar=fg[:, k:k + 1], in1=rstd[:],
                                                                     op0=ALU.mult, op1=ALU.mult),
                             reads=[B_xs4[i], B_rstd, B_const], writes=[B_yT[y]])
                        for tb in range(4):
                            S.tr(PS[:, tb, (k % 4) * 128:(k % 4 + 1) * 128], yT[y][:, tb * 128:(tb + 1) * 128], ident[:],
                                 reads=[B_yT[y], B_const], writes=[Bps[tb]])
                        if k % 4 == 3:
                            k0 = k - 3
                            for tb in range(4):
                                copy_any("act" if tb % 2 else "dve", orow[tb][:, k0 * 128:(k0 + 4) * 128], PS[:, tb, :], [Bps[tb]], [B_orow[tb]])
                    for tb in range(4):
                        row0 = (tt % 2) * 512 + tb * 128
                        dst = (yp_d if tt < 2 else ys_d)[row0:row0 + 128, :]
                        dma_nw(dst, orow[tb][:], [B_orow[tb]], B_out)
                S.barrier()

        if ALL or "in" in stages:
            stage_in()
        if ALL or "mod" in stages:
            stage_mod()
        if ALL or "ffn00" in stages:
            stage_ffn(0, 0)
        if ALL or "win" in stages:
            stage_win()
        if ALL or "kv" in stages:
            stage_kv()
        if ALL or "attn" in stages:
            stage_attn()
        if ALL or "hyena" in stages:
            stage_hyena()
        if ALL or "wout" in stages:
            stage_proj_resid("wo", WOUT_d, 0, B_MIXT)
        if ALL or "ffn01" in stages:
            stage_ffn(0, 1)
        if ALL or "ffn10" in stages:
            stage_ffn(1, 0)
        if ALL or "fnet" in stages:
            stage_fnet()
        if ALL or "wfn" in stages:
            stage_proj_resid("wf", WFN_d, 1, B_MIXT)
        if ALL or "ffn11" in stages:
            stage_ffn(1, 1)
        if ALL or "final" in stages:
            stage_final()

        if "dumpxt" in dbg:
            xt_o = dout("dbg_xt", [D, NTOK])
            with contextlib.ExitStack() as st:
                t = [sb(st, f"dd{i}", [128, NTOK], F32) for i in range(2)]
                Bt = [Buf(f"dd{i}") for i in range(2)]
                for k in range(KC):
                    i = k % 2
                    S.dma(t[i][:], XTv[k], reads=[B_XT[tq][k] for tq in range(NTT)], writes=[Bt[i]])
                    S.dma(xt_o.rearrange("(k p) t -> k p t", p=128)[k], t[i][:], reads=[Bt[i]], writes=[B_out])
        if "dumpmix" in dbg:
            mx_o = dout("dbg_mix", [D, NTOK], BF16)
            with contextlib.ExitStack() as st:
                t = [sb(st, f"dm{i}", [128, NTOK], BF16) for i in range(2)]
                Bt = [Buf(f"dm{i}") for i in range(2)]
                for k in range(KC):
                    i = k % 2
                    S.dma(t[i][:], MIXv[:, k, :], reads=B_MIXT, writes=[Bt[i]])
                    S.dma(mx_o.rearrange("(k p) t -> k p t", p=128)[k], t[i][:], reads=[Bt[i]], writes=[B_out])
        if "dumpmod" in dbg:
            mo = dout("dbg_modp", [128, 2 * 3 * 2 * 3 * 32])
            S.dma(mo, modp[:], reads=[B_modp], writes=[B_out])
        S.barrier()
    return nc


_PREP_CACHE = {}


def _tile_w(w, kc, j):
    return np.ascontiguousarray(w.reshape(kc, 128, j, 128).transpose(2, 1, 0, 3).reshape(j, 128, kc * 128))


def prep_shared(inp):
    sh = {}
    W13 = np.empty((4, JF, 2, 128, D), np.float32)
    W2T = np.empty((4, KC, 128, JF * 128), np.float32)
    for l in range(2):
        for i in range(2):
            W13[l * 2 + i, :, 0] = _tile_w(inp["ffn_w1"][l, i], KC, JF)
            W13[l * 2 + i, :, 1] = _tile_w(inp["ffn_w3"][l, i], KC, JF)
            W2T[l * 2 + i] = _tile_w(inp["ffn_w2"][l, i], JF, KC)
    sh["W13"] = W13
    sh["W2T"] = W2T
    sh["WIN"] = _tile_w(inp["w_in"][0], KC, 96)
    sh["WOUT"] = _tile_w(inp["w_out"][0], KC, 32)
    sh["WFN"] = _tile_w(inp["w_fnet"][0], KC, 32)
    sh["WMOD"] = np.stack([_tile_w(inp["w_mod"][l], KC, 288) for l in range(2)])
    sh["bmodT"] = np.ascontiguousarray(inp["b_mod"].reshape(2, 288, 128).transpose(0, 2, 1))
    g = np.concatenate([inp["norm_g"].reshape(6, D), inp["final_g"].reshape(1, D)], axis=0)
    sh["gT"] = np.ascontiguousarray(g.reshape(7, 32, 128).transpose(2, 0, 1).reshape(128, 7 * 32))
    sh["wsT"] = np.ascontiguousarray(inp["w_short"][0].reshape(3, 48, 128).transpose(2, 1, 0).reshape(128, 144))
    sh["bsT"] = np.ascontiguousarray(inp["b_short"][0].reshape(48, 128).T)
    sh["fw1"] = np.ascontiguousarray(inp["filt_w1"][0])
    sh["fb1"] = np.ascontiguousarray(inp["filt_b1"][0].reshape(64, 1))
    sh["fw2"] = np.ascontiguousarray(inp["filt_w2"][0])
    sh["fb2"] = np.ascontiguousarray(inp["filt_b2"][0].reshape(64, 1))
    sh["ffq"] = np.ascontiguousarray(inp["filt_freq"][0].T)
    sh["fw3"] = np.ascontiguousarray(inp["filt_w3"][0])
    sh["fbias"] = np.ascontiguousarray(inp["filt_bias"][0])
    rp = np.zeros((16, 15, 127), np.float32)
    rp[:, :, 48:79] = inp["rpb"][0]
    sh["rpbp"] = rp
    wq = np.arange(64)[:, None]
    wk = np.arange(64)[None, :]
    idx = wk - wq + 63
    sh["tzh"] = np.ascontiguousarray(rp[:, :, idx].transpose(2, 0, 1, 3).reshape(64, 16 * 15 * 64))
    for k, v in _consts().items():
        sh["c_" + k] = v
    return sh


def core_inputs(inp, sh, ci):
    m = dict(sh)
    m["xp"] = np.ascontiguousarray(inp["x_prompt"][4 * ci:4 * ci + 4].reshape(1024, D))
    m["xs"] = np.ascontiguousarray(inp["x_sample"][ci].reshape(1024, D))
    m["ck"] = np.ascontiguousarray(inp["cache_k"][ci, 0].reshape(512, 2048))
    m["cv"] = np.ascontiguousarray(inp["cache_v"][ci, 0].reshape(512, 2048))
    cond = np.stack([inp["c_ctx"], inp["c"][ci]], axis=0)
    m["condT"] = np.ascontiguousarray(cond.reshape(2, 32, 128).transpose(2, 1, 0).reshape(128, 64))
    return m


def kernel(**inputs):
    inp = {k: np.asarray(v) for k, v in inputs.items()}
    sh = prep_shared(inp)
    nc = build_program()
    in_maps = [core_inputs(inp, sh, ci) for ci in range(NCORES)]
    res = run_bass_kernel_spmd(nc, in_maps, core_ids=list(range(NCORES)))
    r = res.results
    y_prompt = np.stack([r[ci]["yp"].reshape(4, 256, D) for ci in range(NCORES)]).reshape(32, 256, D)
    y_sample = np.stack([r[ci]["ys"] for ci in range(NCORES)]).reshape(8, 1024, D)
    new_k = np.stack([r[ci]["nk"].reshape(4, 1, 256, 16, 128) for ci in range(NCORES)]).reshape(32, 1, 256, 16, 128)
    new_v = np.stack([r[ci]["nv"].reshape(4, 1, 256, 16, 128) for ci in range(NCORES)]).reshape(32, 1, 256, 16, 128)
    return (y_prompt.astype(np.float32), y_sample.astype(np.float32), new_k.astype(np.float32), new_v.astype(np.float32))
```
